# Optimizing a Trainium2 kernel written in Bass

```python
import math
import jax, jax.numpy as jnp
from jax import lax
import numpy as np

D_MODEL = 2048
BATCH = 1
SEQ = 8192
DEPTH = 4

HEAD_DIM = 128
RET_HEADS = D_MODEL // 256
RET_V_HEAD_DIM = 2 * HEAD_DIM
RET_QK = RET_HEADS * HEAD_DIM
RET_V = RET_HEADS * RET_V_HEAD_DIM
DIFF_HEADS = D_MODEL // 256
DIFF_QK = 2 * DIFF_HEADS * HEAD_DIM
DIFF_V = DIFF_HEADS * 2 * HEAD_DIM
D_FF = 4 * D_MODEL
RET_CHUNK = 128
Q_BLOCK = 128
ROPE_THETA = 10000.0
NORM_EPS = 1e-6
GN_EPS = 1e-5
SPLITS = (RET_QK, RET_QK, RET_V, RET_V, DIFF_QK, DIFF_QK, DIFF_V, D_MODEL, D_MODEL)
IN_COLS = sum(SPLITS)

kernel_name = "hybrid_retention_diffattn_encoder"


def rms_norm(x, w):
    xf = x.astype(jnp.float32)
    y = xf * lax.rsqrt(jnp.mean(xf * xf, axis=-1, keepdims=True) + NORM_EPS)
    return (y * w.astype(jnp.float32)).astype(x.dtype)


def head_group_norm(x, w):
    xf = x.astype(jnp.float32)
    mu = jnp.mean(xf, axis=-1, keepdims=True)
    var = jnp.mean(jnp.square(xf - mu), axis=-1, keepdims=True)
    return (xf - mu) * lax.rsqrt(var + GN_EPS) * w.astype(jnp.float32)


def rope_tables(L):
    inv = 1.0 / (ROPE_THETA ** (jnp.arange(0, HEAD_DIM, 2, dtype=jnp.float32) / HEAD_DIM))
    ang = jnp.arange(L, dtype=jnp.float32)[:, None] * inv[None, :]
    ang = jnp.concatenate([ang, ang], axis=-1)
    return jnp.cos(ang), jnp.sin(ang)


def apply_rope(x, cos, sin):
    half = x.shape[-1] // 2
    x1, x2 = x[..., :half], x[..., half:]
    rot = jnp.concatenate([-x2, x1], axis=-1)
    return (x * cos + rot * sin).astype(x.dtype)


def retention_direction(q, k, v, log_g):
    q = q.astype(jnp.float32); k = k.astype(jnp.float32); v = v.astype(jnp.float32)
    B, H, L, dk = q.shape
    dv = v.shape[-1]
    C = RET_CHUNK
    N = L // C
    q = q.reshape(B, H, N, C, dk)
    k = k.reshape(B, H, N, C, dk)
    v = v.reshape(B, H, N, C, dv)
    i = jnp.arange(C, dtype=jnp.float32)
    diff = i[:, None] - i[None, :]
    causal_in_chunk = diff >= 0
    intra_decay = jnp.where(causal_in_chunk[None],
                            jnp.exp(jnp.where(causal_in_chunk, diff, 0.0)[None] * log_g[:, None, None]),
                            0.0)
    s = jnp.einsum('bhncd,bhnmd->bhncm', q, k) * intra_decay[None, :, None]
    intra = jnp.einsum('bhncm,bhnme->bhnce', s, v)
    k_w = k * jnp.exp((C - 1 - i)[None, :] * log_g[:, None])[None, :, None, :, None]
    kv = jnp.einsum('bhncd,bhnce->nbhde', k_w, v)
    chunk_decay = jnp.exp(C * log_g)[None, :, None, None]

    def step(state, kv_n):
        return state * chunk_decay + kv_n, state

    _, state_prev = lax.scan(step, jnp.zeros(kv.shape[1:], jnp.float32), kv)
    q_w = q * jnp.exp((i + 1)[None, :] * log_g[:, None])[None, :, None, :, None]
    cross = jnp.einsum('bhncd,nbhde->bhnce', q_w, state_prev)
    return (intra + cross).reshape(B, H, L, dv)


def bidirectional_retention(q, k, v, logit_fwd, logit_bwd):
    lg_f = jax.nn.log_sigmoid(logit_fwd.astype(jnp.float32))
    lg_b = jax.nn.log_sigmoid(logit_bwd.astype(jnp.float32))
    fwd = retention_direction(q, k, v, lg_f)
    bwd = jnp.flip(retention_direction(jnp.flip(q, 2), jnp.flip(k, 2), jnp.flip(v, 2), lg_b), 2)
    return fwd + bwd


def differential_attention(q, k, v, lam):
    B, H, _, L, d = q.shape
    nb = L // Q_BLOCK
    qb = q.reshape(B, H, 2, nb, Q_BLOCK, d).transpose(3, 0, 1, 2, 4, 5)
    scale = d ** -0.5

    def one_block(q_blk):
        s = jnp.einsum('bhtqd,bhtkd->bhtqk', q_blk, k).astype(jnp.float32) * scale
        p = jax.nn.softmax(s, axis=-1)
        a = (p[:, :, 0] - lam * p[:, :, 1]).astype(v.dtype)
        return jnp.einsum('bhqk,bhke->bhqe', a, v)

    o = lax.map(one_block, qb)
    return o.transpose(1, 2, 0, 3, 4).reshape(B, H, L, v.shape[-1])


def setup_inputs(seed: int = 0) -> dict:
    key = jax.random.key(seed)
    ks = jax.random.split(key, 20)

    def dense(k, shape):
        return jax.random.normal(k, shape, jnp.float32) * (shape[-2] ** -0.5)

    def gain(k, shape):
        return 1.0 + 0.02 * jax.random.normal(k, shape, jnp.float32)

    base_logit = jnp.log(2.0 ** (5.0 + jnp.arange(RET_HEADS, dtype=jnp.float32)) - 1.0)
    return {
        "x": jax.random.normal(ks[0], (BATCH, SEQ, D_MODEL), jnp.float32),
        "norm_mix_w": gain(ks[1], (DEPTH, D_MODEL)),
        "w_in": dense(ks[2], (DEPTH, D_MODEL, IN_COLS)),
        "ret_decay_fwd": base_logit[None] + 0.1 * jax.random.normal(ks[3], (DEPTH, RET_HEADS), jnp.float32),
        "ret_decay_bwd": base_logit[None] + 0.1 * jax.random.normal(ks[4], (DEPTH, RET_HEADS), jnp.float32),
        "ret_gn_w": gain(ks[5], (DEPTH, RET_V)),
        "w_ret_out": dense(ks[6], (DEPTH, RET_V, D_MODEL)),
        "q_norm_w": gain(ks[7], (DEPTH, HEAD_DIM)),
        "k_norm_w": gain(ks[8], (DEPTH, HEAD_DIM)),
        "diff_lambda": 0.1 * jax.random.normal(ks[9], (DEPTH, 4, HEAD_DIM), jnp.float32),
        "diff_subln_w": gain(ks[10], (DEPTH, 2 * HEAD_DIM)),
        "w_diff_out": dense(ks[11], (DEPTH, DIFF_V, D_MODEL)),
        "w_out": dense(ks[12], (DEPTH, D_MODEL, D_MODEL)),
        "norm_mlp_w": gain(ks[13], (DEPTH, D_MODEL)),
        "w_mlp_in": dense(ks[14], (DEPTH, D_MODEL, D_FF)),
        "w_mlp_out": dense(ks[15], (DEPTH, D_FF, D_MODEL)),
    }


def reference(x, norm_mix_w, w_in, ret_decay_fwd, ret_decay_bwd, ret_gn_w, w_ret_out,
              q_norm_w, k_norm_w, diff_lambda, diff_subln_w, w_diff_out, w_out,
              norm_mlp_w, w_mlp_in, w_mlp_out):
    B, L, _ = x.shape
    cos, sin = rope_tables(L)
    split_idx = np.cumsum(SPLITS)[:-1].tolist()
    for l in range(DEPTH):
        h = rms_norm(x, norm_mix_w[l])
        proj = jnp.einsum('bld,de->ble', h, w_in[l])
        rq, rk, rv, rg, dq, dk, dv, gr, gd = jnp.split(proj, split_idx, axis=-1)

        rq = apply_rope(rq.reshape(B, L, RET_HEADS, HEAD_DIM).transpose(0, 2, 1, 3), cos, sin)
        rk = apply_rope(rk.reshape(B, L, RET_HEADS, HEAD_DIM).transpose(0, 2, 1, 3), cos, sin) * (HEAD_DIM ** -0.5)
        rv = rv.reshape(B, L, RET_HEADS, RET_V_HEAD_DIM).transpose(0, 2, 1, 3)
        ret = bidirectional_retention(rq, rk, rv, ret_decay_fwd[l], ret_decay_bwd[l])
        ret = head_group_norm(ret.transpose(0, 2, 1, 3),
                              ret_gn_w[l].reshape(RET_HEADS, RET_V_HEAD_DIM)).reshape(B, L, RET_V).astype(x.dtype)
        y_ret = jnp.einsum('ble,ed->bld', jax.nn.silu(rg) * ret, w_ret_out[l])

        dq = dq.reshape(B, L, DIFF_HEADS, 2, HEAD_DIM).transpose(0, 2, 3, 1, 4)
        dk = dk.reshape(B, L, DIFF_HEADS, 2, HEAD_DIM).transpose(0, 2, 3, 1, 4)
        dq = apply_rope(rms_norm(dq, q_norm_w[l]), cos, sin)
        dk = apply_rope(rms_norm(dk, k_norm_w[l]), cos, sin)
        dv = dv.reshape(B, L, DIFF_HEADS, 2 * HEAD_DIM).transpose(0, 2, 1, 3)
        lambda_init = 0.8 - 0.6 * math.exp(-0.3 * l)
        lp = diff_lambda[l].astype(jnp.float32)
        lam = jnp.exp(jnp.sum(lp[0] * lp[1])) - jnp.exp(jnp.sum(lp[2] * lp[3])) + lambda_init
        o = differential_attention(dq, dk, dv, lam)
        o = rms_norm(o, diff_subln_w[l]) * (1.0 - lambda_init)
        y_diff = jnp.einsum('ble,ed->bld', o.transpose(0, 2, 1, 3).reshape(B, L, DIFF_V), w_diff_out[l])

        merged = jax.nn.sigmoid(gr) * y_ret + jax.nn.sigmoid(gd) * y_diff
        x = x + jnp.einsum('bld,de->ble', merged, w_out[l])

        h = rms_norm(x, norm_mlp_w[l])
        u = jnp.square(jax.nn.relu(jnp.einsum('bld,df->blf', h, w_mlp_in[l])))
        x = x + jnp.einsum('blf,fd->bld', u, w_mlp_out[l])
    return x
```

```python
import math
import contextlib
import numpy as np
import ml_dtypes
import concourse.bass as bass
import concourse.mybir as mybir
from concourse.bass_utils import run_bass_kernel_spmd

F32 = mybir.dt.float32
BF16 = mybir.dt.bfloat16
ALU = mybir.AluOpType
AF = mybir.ActivationFunctionType

NCORES = 8
L = 8192
D = 2048
TOK = L // NCORES
DEPTH = 4
KC = D // 128
NORM_EPS = 1e-6
GN_EPS = 1e-5
SPLITS = (1024, 1024, 2048, 2048, 2048, 2048, 2048, 2048, 2048)
OFF = np.concatenate([[0], np.cumsum(SPLITS)]).tolist()
ATT_SCALE = 128.0 ** -0.5


class Buf:
    __slots__ = ("name", "w", "r", "dsem", "dcnt", "excl")

    def __init__(self, name, dsem=None):
        self.name = name
        self.excl = False
        self.w = None
        self.r = {}
        self.dsem = dsem
        self.dcnt = 0


class Q:
    def __init__(self, name, h, sem, is_pe=False):
        self.name = name
        self.h = h
        self.sem = sem
        self.cnt = 0
        self.seen = {}
        self.pr = []
        self.pw = []
        self.is_pe = is_pe
        self.nwaits = 0
        self.ninst = 0

    def wait(self, ev):
        if ev is None:
            return
        sem, val = ev
        k = id(sem)
        if self.seen.get(k, 0) >= val:
            return
        self.h.wait_ge(sem, val)
        self.nwaits += 1
        self.seen[k] = val


class Sched:
    def __init__(self, nc, sems):
        self.nc = nc
        self._sems = sems
        self.pe = Q("pe", nc.tensor, self.newsem(), is_pe=True)
        self.act = Q("act", nc.scalar, self.newsem())
        self.dve = Q("dve", nc.vector, self.newsem())
        self.pool = Q("pool", nc.gpsimd, self.newsem())
        self.sp = Q("sp", nc.sync, self.newsem())
        self.qs = [self.pe, self.act, self.dve, self.pool, self.sp]
        self.dbufs = []
        self.dcache = {}

    def newsem(self):
        return next(self._sems)

    def buf(self, name, dma=False):
        if dma:
            if name in self.dcache:
                return self.dcache[name]
            b = Buf(name, self.newsem())
            self.dbufs.append(b)
            self.dcache[name] = b
            return b
        return Buf(name, None)

    def op(self, q, fn, reads=(), writes=(), signal=True):
        for b in reads:
            if b.w is not None:
                if b.w[0] is q.sem and q.is_pe:
                    pass
                else:
                    q.wait(b.w)
            if b.excl:
                for ev in b.r.values():
                    if ev[0] is not q.sem:
                        q.wait(ev)
        for b in writes:
            if b.w is not None and not (b.w[0] is q.sem and q.is_pe):
                q.wait(b.w)
            for ev in b.r.values():
                if not (ev[0] is q.sem and q.is_pe):
                    q.wait(ev)
        inst = fn()
        q.ninst += 1
        q.pr.extend(reads)
        q.pw.extend(writes)
        if signal:
            q.cnt += 1
            inst.then_inc(q.sem, 1)
            ev = (q.sem, q.cnt)
            k = id(q.sem)
            for b in q.pw:
                b.w = ev
                b.r = {}
            for b in q.pr:
                if b.w is not ev:
                    b.r[k] = ev
            q.pr = []
            q.pw = []
        return inst

    def dma(self, q, out_buf, out_ap, in_buf, in_ap, **kw):
        assert out_buf.dsem is not None, out_buf.name
        if in_buf is not None and in_buf.w is not None:
            q.wait(in_buf.w)
        if out_buf.w is not None and out_buf.w[0] is not out_buf.dsem:
            q.wait(out_buf.w)
        for ev in out_buf.r.values():
            q.wait(ev)
        inst = q.h.dma_start(out=out_ap, in_=in_ap, **kw)
        inst.then_inc(out_buf.dsem, 16)
        q.ninst += 1
        out_buf.dcnt += 16
        ev = (out_buf.dsem, out_buf.dcnt)
        out_buf.w = ev
        out_buf.r = {}
        if in_buf is not None:
            in_buf.r[id(out_buf.dsem)] = ev
        return inst

    def collective(self, kind, out_buf, out_ap, in_buf, in_ap):
        q = self.pool
        if in_buf.w is not None:
            q.wait(in_buf.w)
        if out_buf.w is not None:
            q.wait(out_buf.w)
        for ev in out_buf.r.values():
            q.wait(ev)
        inst = q.h.collective_compute(
            kind, ALU.bypass, replica_groups=[list(range(NCORES))],
            ins=[in_ap], outs=[out_ap])
        inst.then_inc(out_buf.dsem, 1)
        q.ninst += 1
        out_buf.dcnt += 1
        ev = (out_buf.dsem, out_buf.dcnt)
        out_buf.w = ev
        out_buf.r = {}
        in_buf.r[id(out_buf.dsem)] = ev
        return inst

    def barrier(self):
        for q in self.qs:
            assert not q.pr and not q.pw, q.name
        evs = [(q.sem, q.cnt) for q in self.qs if q.cnt > 0]
        evs += [(b.dsem, b.dcnt) for b in self.dbufs if b.dcnt > 0 and not b.name.startswith("dram:")]
        for q in self.qs:
            for ev in evs:
                if ev[0] is not q.sem:
                    q.wait(ev)


def build(NL=DEPTH, dbg=False, stop=None):
    nc = bass.Bass("TRN2", target_bir_lowering=False)

    def din(name, shape, dt=F32):
        return nc.dram_tensor(name, list(shape), dt, kind="ExternalInput").ap()

    xT_in = din("xT", [128, KC, TOK])
    nw_in = din("nw", [128, 2, DEPTH, KC])
    wh_in = din("wh", [DEPTH, D, 1536])
    WSPEC = {"wg": (D, 4096), "wro": (D, D), "wdo": (D, D), "wo": (D, D), "w1": (D, 4 * D), "w2": (4 * D, D)}
    w_shard_in = {n: din(n + "_s", [DEPTH, R // NCORES, C]) for n, (R, C) in WSPEC.items()}
    w_sb = {n: nc.dram_tensor(n + "_sb", [R // NCORES, C], BF16).ap() for n, (R, C) in WSPEC.items()}
    w_full = {n: [nc.dram_tensor(f"{n}_f{p}", [R, C], BF16).ap() for p in range(2)] for n, (R, C) in WSPEC.items()}
    hp_in = din("hp", [128, DEPTH, 8])
    gnw_in = din("gnw", [128, DEPTH, 256])
    subw_in = din("subw", [128, DEPTH, 256])
    cst_in = din("cst", [128, 4 + 512 + 128])
    cb_in = din("cb", [128, 3, 128], BF16)
    cos_in = din("cosT", [128, L])
    sin_in = din("sinT", [128, L])
    out = nc.dram_tensor("out", [128, KC, TOK], F32, kind="ExternalOutput").ap()
    if dbg:
        d_xn = nc.dram_tensor("d_xn", [D, TOK], BF16, kind="ExternalOutput").ap()
        d_fd = nc.dram_tensor("d_fd", [256, L], BF16, kind="ExternalOutput").ap()
        d_fr = nc.dram_tensor("d_fr", [256, L], BF16, kind="ExternalOutput").ap()
        d_xm = nc.dram_tensor("d_xm", [128, KC, TOK], F32, kind="ExternalOutput").ap()

    xn_loc = nc.dram_tensor("xn_loc", [D, TOK], BF16).ap()
    xn_all = nc.dram_tensor("xn_all", [NCORES * D, TOK], BF16).ap()
    fd_loc = nc.dram_tensor("fd_loc", [256, L], BF16).ap()
    fd_all = nc.dram_tensor("fd_all", [NCORES * 256, L], BF16).ap()
    fr_loc = nc.dram_tensor("fr_loc", [256, L], BF16).ap()
    fr_all = nc.dram_tensor("fr_all", [NCORES * 256, L], BF16).ap()
    sbd = nc.dram_tensor("sbd", [128, 64, 256], BF16).ap()

    es = contextlib.ExitStack()
    uid = [0]

    def uniq(n):
        uid[0] += 1
        return f"{n}_{uid[0]}"

    with es:
        def semgen():
            i = 0
            while True:
                yield es.enter_context(nc.semaphore(f"s{i}"))
                i += 1

        S = Sched(nc, semgen())
        PE, ACT, DVE, POOL, SP = S.pe, S.act, S.dve, S.pool, S.sp

        def pe(fn, r, w, sig=True):
            return S.op(PE, fn, r, w, sig)

        def act(fn, r, w):
            return S.op(ACT, fn, r, w)

        def dve(fn, r, w):
            return S.op(DVE, fn, r, w)

        class Scope:
            def __init__(self):
                self.es = contextlib.ExitStack()

            def sb(self, name, shape, dt):
                return self.es.enter_context(nc.sbuf_tensor(uniq(name), list(shape), dt))

            def close(self):
                S.barrier()
                self.es.close()

        def psb(name, shape, dt):
            return es.enter_context(nc.sbuf_tensor(uniq(name), list(shape), dt))

        xT = psb("xT", [128, KC, TOK], F32)
        BX = [[S.buf(f"x{kc}_{tt}") for tt in range(2)] for kc in range(KC)]
        BXin = S.buf("xin", dma=True)
        cb = psb("cb", [128, 3, 128], BF16)
        cst = psb("cst", [128, 4 + 512 + 128], F32)
        nw = psb("nw", [128, 2, DEPTH, KC], F32)
        hp = psb("hp", [128, DEPTH, 8], F32)
        gnw = psb("gnw", [128, DEPTH, 256], F32)
        subw = psb("subw", [128, DEPTH, 256], F32)
        Bc = S.buf("consts", dma=True)
        ident = cb[:, 0, :]
        ones_b = cb[:, 1, :]
        rotm = cb[:, 2, :]
        ones_f = cst[:, 516:644]

        pf = es.enter_context(nc.psum_tensor("pf", [128, 8, 512], F32))
        PB = [[S.buf(f"ps{b}")] for b in range(8)]
        for pb_ in PB:
            pb_[0].excl = True

        def bank(b):
            return pf[:, b, :], PB[b]

        def half(b, h):
            return pf[:, b, h * 256:(h + 1) * 256], PB[b]

        Bxn_loc = S.buf("xn_loc", dma=True)
        Bxn_all = S.buf("dram:xn_all", dma=True)
        Bfd_loc = S.buf("fd_loc", dma=True)
        Bfd_all = S.buf("dram:fd_all", dma=True)
        Bfr_loc = S.buf("fr_loc", dma=True)
        Bfr_all = S.buf("dram:fr_all", dma=True)
        Bsbd = S.buf("sbd", dma=True)
        Bout = S.buf("out", dma=True)

        for kc in range(0, KC, 4):
            S.dma(SP, BXin, xT[:, kc:kc + 4, :], None, xT_in[:, kc:kc + 4, :])
        for row in BX:
            for b in row:
                b.w = (BXin.dsem, BXin.dcnt)
        S.dma(SP, Bc, cb[:], None, cb_in)
        S.dma(SP, Bc, cst[:], None, cst_in)
        S.dma(SP, Bc, nw[:], None, nw_in)
        S.dma(SP, Bc, hp[:], None, hp_in)
        S.dma(SP, Bc, gnw[:], None, gnw_in)
        S.dma(SP, Bc, subw[:], None, subw_in)

        pid_sp = nc.sync.partition_id()

        xn_loc_v = xn_loc.rearrange("(kc p) t -> p kc t", p=128)
        xn_all_v = xn_all.rearrange("(r kc p) t -> p r kc t", r=NCORES, kc=KC, p=128)
        fd_all_v = fd_all.rearrange("(kc p) t -> p kc t", p=128)
        fr_all_v = fr_all.rearrange("(kc p) t -> p kc t", p=128)
        fd_loc_v = fd_loc.rearrange("(e p) t -> p e t", p=128)
        fr_loc_v = fr_loc.rearrange("(e p) t -> p e t", p=128)

        Bw_sb = {n: S.buf("dram:" + n + "_sb", dma=True) for n in WSPEC}
        Bw_full = {n: [S.buf(f"dram:{n}_f{p}", dma=True) for p in range(2)] for n in WSPEC}

        def prep_weights(ll):
            if ll >= NL:
                return
            for n in WSPEC:
                S.dma(POOL, Bw_sb[n], w_sb[n], None, w_shard_in[n][ll])
            for n in WSPEC:
                S.collective("AllGather", Bw_full[n][ll % 2], w_full[n][ll % 2], Bw_sb[n], w_sb[n])

        def wview(w_ap):
            return w_ap.rearrange("(kc p) n -> p kc n", p=128)

        def norm_tiles(l, which, sc, sink):
            sq = sc.sb("sq", [128, KC, 512], BF16)
            Bsq = S.buf("sq")
            sd = sc.sb("sd", [128, 512], F32)
            Bsd = S.buf("sd")
            rstd = sc.sb("rstd", [128, 512], F32)
            Brs = S.buf("rstd")
            for tt in range(2):
                cols = slice(tt * 512, (tt + 1) * 512)
                xb = [BX[kc][tt] for kc in range(KC)]
                act(lambda: nc.scalar.activation(out=sq[:], in_=xT[:, :, cols], func=AF.Square), xb, [Bsq])
                pa, pbuf = bank(tt)
                for kc in range(KC):
                    pe(lambda: nc.tensor.matmul(pa, lhsT=ones_b, rhs=sq[:, kc, :], start=(kc == 0), stop=(kc == KC - 1)),
                       [Bsq, Bc], pbuf, sig=(kc == KC - 1))
                act(lambda: nc.scalar.activation(out=sd[:], in_=pa, func=AF.Sqrt, scale=1.0 / D, bias=NORM_EPS), pbuf, [Bsd])
                dve(lambda: nc.vector.reciprocal(out=rstd[:], in_=sd[:]), [Bsd], [Brs])
                dst, dbuf = sink(tt)
                for kc in range(KC):
                    dve(lambda: nc.vector.scalar_tensor_tensor(
                        out=dst[:, kc, :], in0=xT[:, kc, cols], scalar=nw[:, which, l, kc:kc + 1], in1=rstd[:],
                        op0=ALU.mult, op1=ALU.mult), [BX[kc][tt], Brs, Bc], [dbuf])
                sink(tt, done=True)

        class NR:
            def __init__(self, sc, n=2):
                self.sets = []
                for i in range(n):
                    t = {}
                    for nm, dt in [("sq", BF16), ("sd", F32), ("rs", F32), ("qn", F32), ("qb", BF16), ("t1", F32), ("t2", F32)]:
                        t[nm] = sc.sb(f"nr_{nm}", [128, 512], dt)
                        t["B" + nm] = S.buf(f"nr_{nm}{i}")
                    self.sets.append(t)
                self.i = 0

            def run(self, ps, psbufs, aux1, aux2, wcol, cos_t, sin_t, Bcs, dst, dbuf, norm=True):
                t = self.sets[self.i % len(self.sets)]
                self.i += 1
                a1, a1b = aux1
                a2, a2b = aux2
                if norm:
                    act(lambda: nc.scalar.activation(out=t["sq"][:], in_=ps, func=AF.Square), psbufs, [t["Bsq"]])
                    pe(lambda: nc.tensor.matmul(a1, lhsT=ones_b, rhs=t["sq"][:], start=True, stop=True), [t["Bsq"], Bc], a1b)
                    act(lambda: nc.scalar.activation(out=t["sd"][:], in_=a1, func=AF.Sqrt, scale=1.0 / 128, bias=NORM_EPS),
                        a1b, [t["Bsd"]])
                    dve(lambda: nc.vector.reciprocal(out=t["rs"][:], in_=t["sd"][:]), [t["Bsd"]], [t["Brs"]])
                    dve(lambda: nc.vector.scalar_tensor_tensor(out=t["qn"][:], in0=ps, scalar=wcol, in1=t["rs"][:],
                                                               op0=ALU.mult, op1=ALU.mult), psbufs + [t["Brs"], Bc], [t["Bqn"]])
                    act(lambda: nc.scalar.copy(out=t["qb"][:], in_=t["qn"][:]), [t["Bqn"]], [t["Bqb"]])
                    src, srcb = t["qn"][:], [t["Bqn"]]
                else:
                    act(lambda: nc.scalar.copy(out=t["qb"][:], in_=ps), psbufs, [t["Bqb"]])
                    src, srcb = ps, psbufs
                pe(lambda: nc.tensor.matmul(a2, lhsT=rotm, rhs=t["qb"][:], start=True, stop=True), [t["Bqb"], Bc], a2b)
                dve(lambda: nc.vector.tensor_tensor(out=t["t1"][:], in0=src, in1=cos_t, op=ALU.mult), srcb + [Bcs], [t["Bt1"]])
                dve(lambda: nc.vector.tensor_tensor(out=t["t2"][:], in0=a2, in1=sin_t, op=ALU.mult), a2b + [Bcs], [t["Bt2"]])
                dve(lambda: nc.vector.tensor_tensor(out=dst, in0=t["t1"][:], in1=t["t2"][:], op=ALU.add),
                    [t["Bt1"], t["Bt2"]], [dbuf])

        class XStream:
            def __init__(self, sc, nslots=1):
                self.xt = [sc.sb("xt", [128, KC, 512], BF16) for _ in range(nslots)]
                self.Bxt = [S.buf(f"xt{i}", dma=True) for i in range(nslots)]
                self.cs = [sc.sb("cs", [128, 2, 512], F32) for _ in range(nslots)]
                self.Bcs = [S.buf(f"cs{i}", dma=True) for i in range(nslots)]
                self.n = nslots
                self.i = 0

            def load(self, T):
                s = self.i % self.n
                self.i += 1
                r, off = T // 2, (T % 2) * 512
                S.dma(SP, self.Bxt[s], self.xt[s][:], Bxn_all, xn_all_v[:, r, :, off:off + 512])
                S.dma(SP, self.Bcs[s], self.cs[s][:, 0, :], None, cos_in[:, T * 512:(T + 1) * 512])
                S.dma(SP, self.Bcs[s], self.cs[s][:, 1, :], None, sin_in[:, T * 512:(T + 1) * 512])
                return self.xt[s], self.Bxt[s], self.cs[s], self.Bcs[s]

        class Panels:
            def __init__(self, sc, n):
                self.t = [sc.sb("pan", [128, KC, 256], BF16) for _ in range(n)]
                self.B = [S.buf(f"pan{i}", dma=True) for i in range(n)]
                self.n = n
                self.i = 0

            def load(self, src_ap, src_buf):
                s = self.i % self.n
                self.i += 1
                S.dma(SP, self.B[s], self.t[s][:], src_buf, src_ap)
                return self.t[s], self.B[s]

        bank_rr = [0]

        def nb(lst):
            b = lst[bank_rr[0] % len(lst)]
            bank_rr[0] += 1
            return b

        for l in range(NL):
            lam_init = 0.8 - 0.6 * math.exp(-0.3 * l)
            par = l % 2

            sc = Scope()
            xnt = [sc.sb("xnt", [128, KC, 512], BF16) for _ in range(2)]
            Bxnt = [S.buf(f"xnt{i}") for i in range(2)]

            def sinkA(tt, done=False):
                if done:
                    S.dma(SP, Bxn_loc, xn_loc_v[:, :, tt * 512:(tt + 1) * 512], Bxnt[tt], xnt[tt][:])
                    return None
                return xnt[tt], Bxnt[tt]

            norm_tiles(l, 0, sc, sinkA)
            S.collective("AllGather", Bxn_all, xn_all, Bxn_loc, xn_loc)
            if dbg and l == 0:
                Bd = S.buf("d_xn", dma=True)
                S.dma(SP, Bd, d_xn, Bxn_loc, xn_loc)
            sc.close()
            if stop == "A":
                break

            sc = Scope()
            kT_all = sc.sb("kT_all", [128, 2, L], BF16)
            BkT = S.buf("kT_all")
            v_all = sc.sb("v_all", [128, 64, 258], BF16)
            Bv = S.buf("v_all")
            dve(lambda: nc.vector.memset(v_all[:, :, 256:257], 1.0), [], [Bv])
            nr = NR(sc, 1)
            xs = XStream(sc, 1)
            sc1 = Scope()
            wh1 = sc1.sb("wh1", [128, KC, 512], BF16)
            Bwh1 = S.buf("wh1", dma=True)
            S.dma(POOL, Bwh1, wh1[:], None, wview(wh_in[l])[:, :, 0:512])
            for T in range(16):
                xt, Bxt, cs, Bcs = xs.load(T)
                for m in range(2):
                    pa, pbuf = bank(m)
                    for kc in range(KC):
                        pe(lambda: nc.tensor.matmul(pa, lhsT=wh1[:, kc, m * 128:(m + 1) * 128], rhs=xt[:, kc, :],
                                                    start=(kc == 0), stop=(kc == KC - 1)), [Bwh1, Bxt], pbuf, sig=(kc == KC - 1))
                for s in range(4):
                    pa, pbuf = half(6 + s // 2, s % 2)
                    for kc in range(KC):
                        pe(lambda: nc.tensor.matmul(pa, lhsT=xt[:, kc, s * 128:(s + 1) * 128], rhs=wh1[:, kc, 256:512],
                                                    start=(kc == 0), stop=(kc == KC - 1)), [Bwh1, Bxt], pbuf, sig=(kc == KC - 1))
                    act(lambda: nc.scalar.copy(out=v_all[:, T * 4 + s, 0:256], in_=pa), pbuf, [Bv])
                for m in range(2):
                    pa, pbuf = bank(m)
                    nr.run(pa, pbuf, bank(2 + m), bank(4 + m), hp[:, l, 3:4], cs[:, 0, :], cs[:, 1, :], Bcs,
                           kT_all[:, m, T * 512:(T + 1) * 512], BkT)
            sc1.close()
            if stop == "B1":
                sc.close()
                break
            sc2 = Scope()
            wh2 = sc2.sb("wh2", [128, KC, 256], BF16)
            Bwh2 = S.buf("wh2", dma=True)
            S.dma(POOL, Bwh2, wh2[:], None, wview(wh_in[l])[:, :, 512:768])
            prep_weights(l)
            qT = sc2.sb("qT", [128, 2, 512], BF16)
            BqT = [S.buf("qT0"), S.buf("qT1")]
            NPT = 4
            pT = [sc2.sb("pT", [128, 512], BF16) for _ in range(NPT)]
            BpT = [S.buf(f"pT{i}") for i in range(NPT)]
            On0 = sc2.sb("On0", [128, 4, 256], F32)
            BOn0 = S.buf("On0")
            oo = sc2.sb("oo", [128, 4, 256], F32)
            Boo = S.buf("oo")
            yb = sc2.sb("yb", [128, 4, 256], BF16)
            Byb = S.buf("yb")
            sm = sc2.sb("sm", [128, 64], F32)
            Bsm = S.buf("sm")
            fst = sc2.sb("fst", [128, 2, 512], BF16)
            Bfst = S.buf("fst")
            subs = sc2.sb("subs", [128, 256], F32)
            Bsubs = S.buf("subs")
            dve(lambda: nc.vector.tensor_tensor(out=sm[:, 0:1], in0=hp[:, l, 4:5], in1=hp[:, l, 5:6], op=ALU.mult), [Bc], [Bsm])
            dve(lambda: nc.vector.tensor_tensor(out=sm[:, 1:2], in0=hp[:, l, 6:7], in1=hp[:, l, 7:8], op=ALU.mult), [Bc], [Bsm])
            pl, plb = half(6, 0)
            lhl = sc2.sb("lhl", [128, 4], BF16)
            Blhl = S.buf("lhl")
            dve(lambda: nc.vector.tensor_copy(out=lhl[:, 0:2], in_=sm[:, 0:2]), [Bsm], [Blhl])
            dve(lambda: nc.vector.tensor_tensor(out=sm[:, 5:7], in0=sm[:, 0:2], in1=lhl[:, 0:2], op=ALU.subtract), [Bsm, Blhl], [Bsm])
            dve(lambda: nc.vector.tensor_copy(out=lhl[:, 2:4], in_=sm[:, 5:7]), [Bsm], [Blhl])
            pe(lambda: nc.tensor.matmul(pl[:, 0:2], lhsT=ones_b, rhs=lhl[:, 0:2], start=True, stop=False), [Blhl, Bc], plb, sig=False)
            pe(lambda: nc.tensor.matmul(pl[:, 0:2], lhsT=ones_b, rhs=lhl[:, 2:4], start=False, stop=True), [Blhl, Bc], plb)
            act(lambda: nc.scalar.activation(out=sm[:, 2:4], in_=pl[:, 0:2], func=AF.Exp), plb, [Bsm])
            dve(lambda: nc.vector.tensor_scalar(out=sm[:, 4:5], in0=sm[:, 3:4], scalar1=sm[:, 2:3], scalar2=-lam_init,
                                                op0=ALU.subtract, op1=ALU.add), [Bsm], [Bsm])
            dve(lambda: nc.vector.tensor_scalar(out=subs[:], in0=subw[:, l, :], scalar1=1.0 - lam_init, scalar2=None,
                                                op0=ALU.mult), [Bc], [Bsubs])
            neglam = sm[:, 4:5]
            for T in range(16):
                xt, Bxt, cs, Bcs = xs.load(T)
                for m in range(2):
                    pa, pbuf = bank(6 + m)
                    for kc in range(KC):
                        pe(lambda: nc.tensor.matmul(pa, lhsT=wh2[:, kc, m * 128:(m + 1) * 128], rhs=xt[:, kc, :],
                                                    start=(kc == 0), stop=(kc == KC - 1)), [Bwh2, Bxt], pbuf, sig=(kc == KC - 1))
                for m in range(2):
                    pa, pbuf = bank(6 + m)
                    nr.run(pa, pbuf, bank(4), bank(5), hp[:, l, 2:3], cs[:, 0, :], cs[:, 1, :], Bcs, qT[:, m, :], BqT[m])
                NS, LA = 4, 3
                for m in range(2):
                    def emit_qk(kq):
                        sq_, sqb_ = bank(4 + kq % NS)
                        pe(lambda: nc.tensor.matmul(sq_, lhsT=kT_all[:, m, kq * 128:(kq + 1) * 128], rhs=qT[:, m, :],
                                                    start=True, stop=True), [BkT, BqT[m]], sqb_)
                    for kq in range(LA):
                        emit_qk(kq)
                    for kb in range(64):
                        if kb + LA < 64:
                            emit_qk(kb + LA)
                        sa, sbuf_ = bank(4 + kb % NS)
                        pi = kb % NPT
                        act(lambda: nc.scalar.activation(out=pT[pi][:], in_=sa, func=AF.Exp, scale=ATT_SCALE), sbuf_, [BpT[pi]])
                        for s in range(4):
                            oa, obuf = bank(s)
                            pe(lambda: nc.tensor.matmul(oa[:, 0:257], lhsT=pT[pi][:, s * 128:(s + 1) * 128], rhs=v_all[:, kb, 0:257],
                                                        start=(kb == 0), stop=(kb == 63)), [BpT[pi], Bv], obuf, sig=(s == 3))
                    for s in range(4):
                        oa, obuf = bank(s)
                        c0 = 8 + m * 8 + s
                        dve(lambda: nc.vector.reciprocal(out=sm[:, c0:c0 + 1], in_=oa[:, 256:257]), obuf, [Bsm])
                        if m == 0:
                            act(lambda: nc.scalar.activation(out=On0[:, s, :], in_=oa[:, 0:256], func=AF.Identity, scale=sm[:, c0:c0 + 1]),
                                obuf + [Bsm], [BOn0])
                        else:
                            c1 = 24 + s
                            dve(lambda: nc.vector.tensor_tensor(out=sm[:, c1:c1 + 1], in0=sm[:, c0:c0 + 1], in1=neglam, op=ALU.mult),
                                [Bsm], [Bsm])
                            dve(lambda: nc.vector.scalar_tensor_tensor(out=oo[:, s, :], in0=oa[:, 0:256], scalar=sm[:, c1:c1 + 1],
                                                                       in1=On0[:, s, :], op0=ALU.mult, op1=ALU.add),
                                obuf + [Bsm, BOn0], [Boo])
                            c2 = 32 + s * 8
                            dve(lambda: nc.vector.bn_stats(out=sm[:, c2:c2 + 6], in_=oo[:, s, :]), [Boo], [Bsm])
                            dve(lambda: nc.vector.bn_aggr(out=sm[:, c2 + 6:c2 + 8], in_=sm[:, c2:c2 + 6]), [Bsm], [Bsm])
                            dve(lambda: nc.vector.scalar_tensor_tensor(out=sm[:, c2:c2 + 1], in0=sm[:, c2 + 6:c2 + 7],
                                                                       scalar=sm[:, c2 + 6:c2 + 7], in1=sm[:, c2 + 7:c2 + 8],
                                                                       op0=ALU.mult, op1=ALU.add), [Bsm], [Bsm])
                            act(lambda: nc.scalar.activation(out=sm[:, c2 + 1:c2 + 2], in_=sm[:, c2:c2 + 1], func=AF.Sqrt,
                                                             scale=1.0, bias=NORM_EPS), [Bsm], [Bsm])
                            dve(lambda: nc.vector.reciprocal(out=sm[:, c2 + 2:c2 + 3], in_=sm[:, c2 + 1:c2 + 2]), [Bsm], [Bsm])
                            dve(lambda: nc.vector.scalar_tensor_tensor(out=yb[:, s, :], in0=oo[:, s, :], scalar=sm[:, c2 + 2:c2 + 3],
                                                                       in1=subs[:], op0=ALU.mult, op1=ALU.mult),
                                [Boo, Bsm, Bsubs], [Byb])
                for e in range(2):
                    ta, tbuf = half(6 + e, 0)
                    tav = ta.bitcast(BF16)
                    for s in range(4):
                        pe(lambda: nc.tensor.transpose(tav[:, s * 128:(s + 1) * 128], yb[:, s, e * 128:(e + 1) * 128], ident),
                           [Byb, Bc], tbuf, sig=(s == 3))
                    act(lambda: nc.scalar.copy(out=fst[:, e, :], in_=tav), tbuf, [Bfst])
                S.dma(SP, Bfd_loc, fd_loc_v[:, :, T * 512:(T + 1) * 512], Bfst, fst[:])
            S.collective("AllGather", Bfd_all, fd_all, Bfd_loc, fd_loc)
            if dbg and l == 0:
                Bd = S.buf("d_fd", dma=True)
                S.dma(SP, Bd, d_fd, Bfd_loc, fd_loc)
            sc2.close()
            sc.close()
            if stop == "B":
                break

            sc = Scope()
            rkT = sc.sb("rkT", [128, L], BF16)
            BrkT = S.buf("rkT")
            rv = sc.sb("rv", [128, 64, 256], BF16)
            Brv = S.buf("rv")
            nr = NR(sc, 1)
            xs = XStream(sc, 1)
            rc = sc.sb("rc", [128, 16], F32)
            Brc = S.buf("rc")
            mask = sc.sb("mask", [128, 128], F32)
            Bmask = S.buf("mask")
            mtmp = sc.sb("mtmp", [128, 2, 128], F32)
            Bmt = S.buf("mtmp")
            st_f = sc.sb("st_f", [128, 256], F32)
            st_b = sc.sb("st_b", [128, 256], F32)
            Bstf, Bstb = S.buf("st_f"), S.buf("st_b")
            stb16 = [sc.sb("stb16", [128, 256], BF16) for _ in range(2)]
            Bstb16 = [S.buf(f"stb16_{i}") for i in range(2)]
            stf16 = [sc.sb("stf16", [128, 256], BF16) for _ in range(2)]
            Bstf16 = [S.buf(f"stf16_{i}") for i in range(2)]
            kcb = [sc.sb("kcb", [128, 128], BF16) for _ in range(2)]
            Bkcb = [S.buf(f"kcb{i}") for i in range(2)]
            act(lambda: nc.scalar.activation(out=rc[:, 0:2], in_=hp[:, l, 0:2], func=AF.Exp, scale=-1.0), [Bc], [Brc])
            act(lambda: nc.scalar.activation(out=rc[:, 2:4], in_=rc[:, 0:2], func=AF.Ln, scale=1.0, bias=1.0), [Brc], [Brc])
            dve(lambda: nc.vector.tensor_scalar(out=rc[:, 4:6], in0=rc[:, 2:4], scalar1=-1.0, scalar2=None, op0=ALU.mult), [Brc], [Brc])
            lgf, lgb = rc[:, 4:5], rc[:, 5:6]
            act(lambda: nc.scalar.activation(out=rc[:, 6:7], in_=cst[:, 0:1], func=AF.Exp, scale=lgf), [Bc, Brc], [Brc])
            act(lambda: nc.scalar.activation(out=rc[:, 7:8], in_=cst[:, 1:2], func=AF.Exp, scale=lgf), [Bc, Brc], [Brc])
            act(lambda: nc.scalar.activation(out=rc[:, 8:9], in_=cst[:, 2:3], func=AF.Exp, scale=lgb), [Bc, Brc], [Brc])
            act(lambda: nc.scalar.activation(out=rc[:, 9:10], in_=cst[:, 3:4], func=AF.Exp, scale=lgb), [Bc, Brc], [Brc])
            dve(lambda: nc.vector.tensor_scalar(out=rc[:, 7:8], in0=rc[:, 7:8], scalar1=ATT_SCALE, scalar2=None, op0=ALU.mult), [Brc], [Brc])
            dve(lambda: nc.vector.tensor_scalar(out=rc[:, 9:10], in0=rc[:, 9:10], scalar1=ATT_SCALE, scalar2=None, op0=ALU.mult), [Brc], [Brc])
            act(lambda: nc.scalar.activation(out=rc[:, 10:12], in_=rc[:, 4:6], func=AF.Exp, scale=128.0), [Brc], [Brc])
            gf_q, gf_k, gb_q, gb_k, gfC, gbC = (rc[:, 6:7], rc[:, 7:8], rc[:, 8:9], rc[:, 9:10], rc[:, 10:11], rc[:, 11:12])
            act(lambda: nc.scalar.activation(out=mtmp[:, 0, :], in_=cst[:, 4:132], func=AF.Exp, scale=lgf), [Bc, Brc], [Bmt])
            act(lambda: nc.scalar.activation(out=mtmp[:, 1, :], in_=cst[:, 260:388], func=AF.Exp, scale=lgb), [Bc, Brc], [Bmt])
            dve(lambda: nc.vector.tensor_tensor(out=mtmp[:, 0, :], in0=mtmp[:, 0, :], in1=cst[:, 132:260], op=ALU.mult), [Bmt, Bc], [Bmt])
            dve(lambda: nc.vector.tensor_tensor(out=mtmp[:, 1, :], in0=mtmp[:, 1, :], in1=cst[:, 388:516], op=ALU.mult), [Bmt, Bc], [Bmt])
            dve(lambda: nc.vector.tensor_tensor(out=mask[:], in0=mtmp[:, 0, :], in1=mtmp[:, 1, :], op=ALU.add), [Bmt], [Bmask])
            dve(lambda: nc.vector.tensor_scalar(out=mask[:], in0=mask[:], scalar1=ATT_SCALE, scalar2=None, op0=ALU.mult), [Bmask], [Bmask])
            dve(lambda: nc.vector.memset(st_f[:], 0.0), [], [Bstf])
            dve(lambda: nc.vector.memset(st_b[:], 0.0), [], [Bstb])

            def kv_update(n, kvec, st, Bst, gC, ci):
                ta, tbuf = half(6, ci % 2)
                tav = ta.bitcast(BF16)
                pe(lambda: nc.tensor.transpose(tav[:, 0:128], rkT[:, n * 128:(n + 1) * 128], ident), [BrkT, Bc], tbuf)
                kk, Bkk = kcb[ci % 2], Bkcb[ci % 2]
                act(lambda: nc.scalar.activation(out=kk[:], in_=tav[:, 0:128], func=AF.Identity, scale=kvec), tbuf + [Brc], [Bkk])
                ka, kbuf = half(7, ci % 2)
                pe(lambda: nc.tensor.matmul(ka, lhsT=kk[:], rhs=rv[:, n, :], start=True, stop=True), [Bkk, Brv], kbuf)
                dve(lambda: nc.vector.scalar_tensor_tensor(out=st[:], in0=st[:], scalar=gC, in1=ka, op0=ALU.mult, op1=ALU.add),
                    [Bst, Brc] + kbuf, [Bst])

            sc1 = Scope()
            wh3 = sc1.sb("wh3", [128, KC, 384], BF16)
            Bwh3 = S.buf("wh3", dma=True)
            S.dma(POOL, Bwh3, wh3[:], None, wview(wh_in[l])[:, :, 768:1152])
            sst = [sc1.sb("sst", [128, 4, 256], BF16) for _ in range(2)]
            Bsst = [S.buf(f"sst{i}") for i in range(2)]
            ci = 0
            for T in reversed(range(16)):
                xt, Bxt, cs, Bcs = xs.load(T)
                pa, pbuf = bank(0)
                for kc in range(KC):
                    pe(lambda: nc.tensor.matmul(pa, lhsT=wh3[:, kc, 0:128], rhs=xt[:, kc, :], start=(kc == 0), stop=(kc == KC - 1)),
                       [Bwh3, Bxt], pbuf, sig=(kc == KC - 1))
                for s in range(4):
                    va, vbuf = half(4 + s // 2, s % 2)
                    for kc in range(KC):
                        pe(lambda: nc.tensor.matmul(va, lhsT=xt[:, kc, s * 128:(s + 1) * 128], rhs=wh3[:, kc, 128:384],
                                                    start=(kc == 0), stop=(kc == KC - 1)), [Bwh3, Bxt], vbuf, sig=(kc == KC - 1))
                    act(lambda: nc.scalar.copy(out=rv[:, T * 4 + s, :], in_=va), vbuf, [Brv])
                nr.run(pa, pbuf, bank(2), bank(3), None, cs[:, 0, :], cs[:, 1, :], Bcs, rkT[:, T * 512:(T + 1) * 512], BrkT, norm=False)
                ss, Bss = sst[T % 2], Bsst[T % 2]
                for s in reversed(range(4)):
                    n = T * 4 + s
                    act(lambda: nc.scalar.copy(out=ss[:, s, :], in_=st_b[:]), [Bstb], [Bss])
                    kv_update(n, gb_k, st_b, Bstb, gbC, ci)
                    ci += 1
                S.dma(SP, Bsbd, sbd[:, T * 4:(T + 1) * 4, :], Bss, ss[:])
            sc1.close()
            sc2 = Scope()
            wh4 = sc2.sb("wh4", [128, KC, 384], BF16)
            Bwh4 = S.buf("wh4", dma=True)
            S.dma(POOL, Bwh4, wh4[:], None, wview(wh_in[l])[:, :, 1152:1536])
            rq = sc2.sb("rq", [128, 512], BF16)
            Brq = S.buf("rq")
            sg = sc2.sb("sg", [128, 4, 256], F32)
            Bsg = S.buf("sg")
            gs = sc2.sb("gs", [128, 4, 256], F32)
            Bgs = S.buf("gs")
            sbt = [sc2.sb("sbt", [128, 4, 256], BF16) for _ in range(2)]
            Bsbt = [S.buf(f"sbt{i}", dma=True) for i in range(2)]
            smk = [sc2.sb("smk", [128, 128], BF16) for _ in range(2)]
            Bsmk = [S.buf(f"smk{i}") for i in range(2)]
            ow = [sc2.sb("ow", [128, 3, 256], F32) for _ in range(2)]
            Bow = [S.buf(f"ow{i}") for i in range(2)]
            gsm = sc2.sb("gsm", [128, 2, 16], F32)
            Bgsm = [S.buf("gsm0"), S.buf("gsm1")]
            y3 = [sc2.sb("y3", [128, 256], BF16) for _ in range(2)]
            By3 = [S.buf(f"y3_{i}") for i in range(2)]
            fst = sc2.sb("fstr", [128, 2, 512], BF16)
            Bfst = S.buf("fstr")
            dve(lambda: nc.vector.memset(stf16[1][:], 0.0), [], [Bstf16[1]])
            ci = 0
            for T in range(16):
                xt, Bxt, cs, Bcs = xs.load(T)
                S.dma(SP, Bsbt[T % 2], sbt[T % 2][:], Bsbd, sbd[:, T * 4:(T + 1) * 4, :])
                pa, pbuf = bank(0)
                for kc in range(KC):
                    pe(lambda: nc.tensor.matmul(pa, lhsT=wh4[:, kc, 0:128], rhs=xt[:, kc, :], start=(kc == 0), stop=(kc == KC - 1)),
                       [Bwh4, Bxt], pbuf, sig=(kc == KC - 1))
                for s in range(4):
                    ga, gbuf = half(1, s % 2)
                    for kc in range(KC):
                        pe(lambda: nc.tensor.matmul(ga, lhsT=xt[:, kc, s * 128:(s + 1) * 128], rhs=wh4[:, kc, 128:384],
                                                    start=(kc == 0), stop=(kc == KC - 1)), [Bwh4, Bxt], gbuf, sig=(kc == KC - 1))
                    act(lambda: nc.scalar.activation(out=sg[:, s, :], in_=ga, func=AF.Silu), gbuf, [Bsg])
                nr.run(pa, pbuf, bank(2), bank(3), None, cs[:, 0, :], cs[:, 1, :], Bcs, rq[:], Brq, norm=False)
                for s in range(4):
                    dve(lambda: nc.vector.tensor_tensor(out=gs[:, s, :], in0=sg[:, s, :], in1=gnw[:, l, :], op=ALU.mult),
                        [Bsg, Bc], [Bgs])
                for s in range(4):
                    n = T * 4 + s
                    i2 = ci % 2
                    qc = rq[:, s * 128:(s + 1) * 128]
                    sa, sbuf_ = half(4, i2)
                    pe(lambda: nc.tensor.matmul(sa[:, 0:128], lhsT=rkT[:, n * 128:(n + 1) * 128], rhs=qc, start=True, stop=True),
                       [BrkT, Brq], sbuf_)
                    dve(lambda: nc.vector.tensor_tensor(out=smk[i2][:], in0=sa[:, 0:128], in1=mask[:], op=ALU.mult),
                        sbuf_ + [Bmask], [Bsmk[i2]])
                    oa, obuf = half(5, i2)
                    pe(lambda: nc.tensor.matmul(oa, lhsT=smk[i2][:], rhs=rv[:, n, :], start=True, stop=True), [Bsmk[i2], Brv], obuf)
                    ua, ubuf = half(2, i2)
                    pe(lambda: nc.tensor.matmul(ua, lhsT=qc, rhs=stf16[(ci + 1) % 2][:], start=True, stop=True),
                       [Brq, Bstf16[(ci + 1) % 2]], ubuf)
                    wa, wbuf = half(3, i2)
                    pe(lambda: nc.tensor.matmul(wa, lhsT=qc, rhs=sbt[T % 2][:, s, :], start=True, stop=True), [Brq, Bsbt[T % 2]], wbuf)
                    o_ = ow[i2]
                    act(lambda: nc.scalar.copy(out=o_[:, 0, :], in_=oa), obuf, [Bow[i2]])
                    dve(lambda: nc.vector.scalar_tensor_tensor(out=o_[:, 1, :], in0=ua, scalar=gf_q, in1=o_[:, 0, :], op0=ALU.mult, op1=ALU.add),
                        ubuf + [Brc, Bow[i2]], [Bow[i2]])
                    dve(lambda: nc.vector.scalar_tensor_tensor(out=o_[:, 2, :], in0=wa, scalar=gb_q, in1=o_[:, 1, :], op0=ALU.mult, op1=ALU.add),
                        wbuf + [Brc, Bow[i2]], [Bow[i2]])
                    g_ = gsm[:, i2, :]
                    dve(lambda: nc.vector.bn_stats(out=g_[:, 0:6], in_=o_[:, 2, :]), [Bow[i2]], [Bgsm[i2]])
                    dve(lambda: nc.vector.bn_aggr(out=g_[:, 6:8], in_=g_[:, 0:6]), [Bgsm[i2]], [Bgsm[i2]])
                    act(lambda: nc.scalar.activation(out=g_[:, 8:9], in_=g_[:, 7:8], func=AF.Sqrt, scale=1.0, bias=GN_EPS), [Bgsm[i2]], [Bgsm[i2]])
                    dve(lambda: nc.vector.reciprocal(out=g_[:, 9:10], in_=g_[:, 8:9]), [Bgsm[i2]], [Bgsm[i2]])
                    dve(lambda: nc.vector.scalar_tensor_tensor(out=o_[:, 0, :], in0=o_[:, 2, :], scalar=g_[:, 6:7], in1=gs[:, s, :],
                                                               op0=ALU.subtract, op1=ALU.mult), [Bow[i2], Bgsm[i2], Bgs], [Bow[i2]])
                    act(lambda: nc.scalar.activation(out=y3[i2][:], in_=o_[:, 0, :], func=AF.Identity, scale=g_[:, 9:10]),
                        [Bow[i2], Bgsm[i2]], [By3[i2]])
                    for e in range(2):
                        ta, tbuf = half(0, e)
                        tav = ta.bitcast(BF16)
                        pe(lambda: nc.tensor.transpose(tav[:, 0:128], y3[i2][:, e * 128:(e + 1) * 128], ident), [By3[i2], Bc], tbuf)
                        act(lambda: nc.scalar.copy(out=fst[:, e, s * 128:(s + 1) * 128], in_=tav[:, 0:128]), tbuf, [Bfst])
                    kv_update(n, gf_k, st_f, Bstf, gfC, ci)
                    act(lambda: nc.scalar.copy(out=stf16[ci % 2][:], in_=st_f[:]), [Bstf], [Bstf16[ci % 2]])
                    ci += 1
                S.dma(SP, Bfr_loc, fr_loc_v[:, :, T * 512:(T + 1) * 512], Bfst, fst[:])
            S.collective("AllGather", Bfr_all, fr_all, Bfr_loc, fr_loc)
            if dbg and l == 0:
                Bd = S.buf("d_fr", dma=True)
                S.dma(SP, Bd, d_fr, Bfr_loc, fr_loc)
            sc2.close()
            sc.close()
            if stop == "C":
                break

            sc = Scope()
            pan = Panels(sc, 6)
            fr_t = sc.sb("fr_t", [128, KC, 512], BF16)
            fd_t = sc.sb("fd_t", [128, KC, 512], BF16)
            xo_t = sc.sb("xo_t", [128, KC, 512], BF16)
            Bfr_t, Bfd_t, Bxo_t = S.buf("fr_t", dma=True), S.buf("fd_t", dma=True), S.buf("xo_t", dma=True)
            mg = sc.sb("mg", [128, KC, 512], BF16)
            Bmg = S.buf("mg")
            sig = [sc.sb("sig", [128, 2, 512], F32) for _ in range(2)]
            Bsig = [S.buf(f"sig{i}") for i in range(2)]
            mm = [sc.sb("mm", [128, 2, 512], F32)] * 2
            Bmm = [S.buf("mm0")] * 2
            it = 0
            for tt in range(2):
                cols = slice(tt * 512, (tt + 1) * 512)
                S.dma(SP, Bfr_t, fr_t[:], Bfr_all, fr_all_v[:, :, bass.ds(pid_sp * TOK + tt * 512, 512)])
                S.dma(SP, Bfd_t, fd_t[:], Bfd_all, fd_all_v[:, :, bass.ds(pid_sp * TOK + tt * 512, 512)])
                S.dma(SP, Bxo_t, xo_t[:], Bxn_loc, xn_loc_v[:, :, cols])
                for jp in range(8):
                    c0 = jp * 256
                    srcs = [("wro", c0, fr_t, Bfr_t), ("wdo", c0, fd_t, Bfd_t), ("wg", c0, xo_t, Bxo_t), ("wg", 2048 + c0, xo_t, Bxo_t)]
                    pans = [pan.load(wview(w_full[wn][par])[:, :, cc:cc + 256], Bw_full[wn][par]) + (rt, Brt) for wn, cc, rt, Brt in srcs]
                    for jj in range(2):
                        j = jp * 2 + jj
                        base = (it % 2) * 4
                        i2 = it % 2
                        it += 1
                        for g, (pt, Bpt, rt, Brt) in enumerate(pans):
                            pa, pbuf = bank(base + g)
                            for kc in range(KC):
                                pe(lambda: nc.tensor.matmul(pa, lhsT=pt[:, kc, jj * 128:(jj + 1) * 128], rhs=rt[:, kc, :],
                                                            start=(kc == 0), stop=(kc == KC - 1)), [Bpt, Brt], pbuf, sig=(kc == KC - 1))
                        act(lambda: nc.scalar.activation(out=sig[i2][:, 0, :], in_=pf[:, base + 2, :], func=AF.Sigmoid), PB[base + 2], [Bsig[i2]])
                        act(lambda: nc.scalar.activation(out=sig[i2][:, 1, :], in_=pf[:, base + 3, :], func=AF.Sigmoid), PB[base + 3], [Bsig[i2]])
                        dve(lambda: nc.vector.tensor_tensor(out=mm[i2][:, 0, :], in0=pf[:, base + 0, :], in1=sig[i2][:, 0, :], op=ALU.mult),
                            PB[base + 0] + [Bsig[i2]], [Bmm[i2]])
                        dve(lambda: nc.vector.tensor_tensor(out=mm[i2][:, 1, :], in0=pf[:, base + 1, :], in1=sig[i2][:, 1, :], op=ALU.mult),
                            PB[base + 1] + [Bsig[i2]], [Bmm[i2]])
                        dve(lambda: nc.vector.tensor_tensor(out=mg[:, j, :], in0=mm[i2][:, 0, :], in1=mm[i2][:, 1, :], op=ALU.add),
                            [Bmm[i2]], [Bmg])
                for jp in range(8):
                    c0 = jp * 256
                    pt, Bpt = pan.load(wview(w_full["wo"][par])[:, :, c0:c0 + 256], Bw_full["wo"][par])
                    for jj in range(2):
                        j = jp * 2 + jj
                        b = nb(list(range(8)))
                        pa, pbuf = bank(b)
                        for kc in range(KC):
                            pe(lambda: nc.tensor.matmul(pa, lhsT=pt[:, kc, jj * 128:(jj + 1) * 128], rhs=mg[:, kc, :],
                                                        start=(kc == 0), stop=(kc == KC - 1)), [Bpt, Bmg], pbuf, sig=(kc == KC - 1))
                        dve(lambda: nc.vector.tensor_tensor(out=xT[:, j, cols], in0=xT[:, j, cols], in1=pa, op=ALU.add),
                            [BX[j][tt]] + pbuf, [BX[j][tt]])
            if dbg and l == 0:
                Bd = S.buf("d_xm", dma=True)
                allx = [b for row in BX for b in row]
                for b in allx:
                    SP.wait(b.w)
                S.dma(SP, Bd, d_xm, None, xT[:])
                for b in allx:
                    b.r[id(Bd.dsem)] = Bd.w
            sc.close()
            if stop == "D":
                break

            sc = Scope()
            hn = sc.sb("hn", [128, KC, TOK], BF16)
            Bhn = [S.buf("hn0"), S.buf("hn1")]
            scn = Scope()

            def sinkE(tt, done=False):
                if done:
                    return None
                return hn[:, :, tt * 512:(tt + 1) * 512], Bhn[tt]

            norm_tiles(l, 1, scn, sinkE)
            scn.close()
            pan = Panels(sc, 6)
            uT = sc.sb("uT", [128, KC, TOK], BF16)
            BuT = [S.buf("uT0"), S.buf("uT1")]
            rl = [sc.sb("rl", [128, 512], F32) for _ in range(3)]
            Brl = [S.buf(f"rl{i}") for i in range(3)]
            ri = 0
            for fg in range(4):
                for jp in range(8):
                    c0 = fg * 2048 + jp * 256
                    pt, Bpt = pan.load(wview(w_full["w1"][par])[:, :, c0:c0 + 256], Bw_full["w1"][par])
                    for jj in range(2):
                        j = jp * 2 + jj
                        for tt in range(2):
                            cols = slice(tt * 512, (tt + 1) * 512)
                            pa, pbuf = bank(nb(list(range(8))))
                            for kc in range(KC):
                                pe(lambda: nc.tensor.matmul(pa, lhsT=pt[:, kc, jj * 128:(jj + 1) * 128], rhs=hn[:, kc, cols],
                                                            start=(kc == 0), stop=(kc == KC - 1)), [Bpt, Bhn[tt]], pbuf, sig=(kc == KC - 1))
                            r_, Br_ = rl[ri % 3], Brl[ri % 3]
                            ri += 1
                            act(lambda: nc.scalar.activation(out=r_[:], in_=pa, func=AF.Relu), pbuf, [Br_])
                            dve(lambda: nc.vector.tensor_tensor(out=uT[:, j, cols], in0=r_[:], in1=r_[:], op=ALU.mult), [Br_], [BuT[tt]])
                for jp in range(8):
                    c0 = jp * 256
                    pt, Bpt = pan.load(w_full["w2"][par][fg * 2048:(fg + 1) * 2048, :].rearrange("(kc p) n -> p kc n", p=128)[:, :, c0:c0 + 256],
                                       Bw_full["w2"][par])
                    for jj in range(2):
                        j = jp * 2 + jj
                        for tt in range(2):
                            cols = slice(tt * 512, (tt + 1) * 512)
                            pa, pbuf = bank(nb(list(range(8))))
                            for kc in range(KC):
                                pe(lambda: nc.tensor.matmul(pa, lhsT=pt[:, kc, jj * 128:(jj + 1) * 128], rhs=uT[:, kc, cols],
                                                            start=(kc == 0), stop=(kc == KC - 1)), [Bpt, BuT[tt]], pbuf, sig=(kc == KC - 1))
                            dve(lambda: nc.vector.tensor_tensor(out=xT[:, j, cols], in0=xT[:, j, cols], in1=pa, op=ALU.add),
                                [BX[j][tt]] + pbuf, [BX[j][tt]])
            sc.close()

        for row in BX:
            for b in row:
                SP.wait(b.w)
        for kc in range(0, KC, 4):
            S.dma(SP, Bout, out[:, kc:kc + 4, :], None, xT[:, kc:kc + 4, :])
        SP.wait(Bout.w)
        for b in S.dbufs:
            if b.dcnt > 0:
                SP.wait((b.dsem, b.dcnt))
        stats = {q.name: (q.ninst, q.nwaits) for q in S.qs}
    return nc, stats


def _consts():
    p = np.arange(128, dtype=np.float32)
    cst = np.zeros((128, 4 + 512 + 128), np.float32)
    cst[:, 0] = p + 1
    cst[:, 1] = 127 - p
    cst[:, 2] = 128 - p
    cst[:, 3] = p
    m = p[:, None]
    c = p[None, :]
    cst[:, 4:132] = np.maximum(c - m, 0)
    cst[:, 132:260] = (c >= m)
    cst[:, 260:388] = np.maximum(m - c, 0)
    cst[:, 388:516] = (m >= c)
    cst[:, 516:644] = 1.0
    cb = np.zeros((128, 3, 128), np.float32)
    cb[:, 0, :] = np.eye(128)
    cb[:, 1, :] = 1.0
    for d in range(128):
        if d < 64:
            cb[d + 64, 2, d] = -1.0
        else:
            cb[d - 64, 2, d] = 1.0
    inv = (1.0 / (10000.0 ** (np.arange(0, 128, 2, dtype=np.float32) / np.float32(128)))).astype(np.float32)
    ang = (np.arange(L, dtype=np.float32)[:, None] * inv[None, :]).astype(np.float32)
    ang = np.concatenate([ang, ang], axis=-1)
    cosT = np.ascontiguousarray(np.cos(ang.astype(np.float64)).astype(np.float32).T)
    sinT = np.ascontiguousarray(np.sin(ang.astype(np.float64)).astype(np.float32).T)
    return cst, cb.astype(ml_dtypes.bfloat16), cosT, sinT


def _to_fm(a):
    return np.ascontiguousarray(a.T.reshape(KC, 128, -1).transpose(1, 0, 2))


def make_in_maps(x, norm_mix_w, w_in, ret_decay_fwd, ret_decay_bwd, ret_gn_w, w_ret_out,
                 q_norm_w, k_norm_w, diff_lambda, diff_subln_w, w_diff_out, w_out,
                 norm_mlp_w, w_mlp_in, w_mlp_out):
    f = lambda a: np.asarray(a, dtype=np.float32)
    x, norm_mix_w, w_in = f(x), f(norm_mix_w), f(w_in)
    cst, cb, cosT, sinT = _consts()
    nw = np.stack([f(norm_mix_w), f(norm_mlp_w)], axis=0)
    nw = np.ascontiguousarray(nw.reshape(2, DEPTH, KC, 128).transpose(3, 0, 1, 2))
    full = {"wg": w_in[:, :, OFF[7]:OFF[9]], "wro": f(w_ret_out), "wdo": f(w_diff_out), "wo": f(w_out),
            "w1": f(w_mlp_in), "w2": f(w_mlp_out)}
    subw = np.ascontiguousarray(np.broadcast_to(f(diff_subln_w)[None], (128, DEPTH, 256)))
    qn, kn, dl = f(q_norm_w), f(k_norm_w), f(diff_lambda)
    rdf, rdb, gn = f(ret_decay_fwd), f(ret_decay_bwd), f(ret_gn_w)
    maps = []
    for c in range(NCORES):
        xc = _to_fm(x[0, c * TOK:(c + 1) * TOK, :])
        cols = np.concatenate([
            np.arange(OFF[5] + c * 256, OFF[5] + (c + 1) * 256),
            np.arange(OFF[6] + c * 256, OFF[6] + (c + 1) * 256),
            np.arange(OFF[4] + c * 256, OFF[4] + (c + 1) * 256),
            np.arange(OFF[1] + c * 128, OFF[1] + (c + 1) * 128),
            np.arange(OFF[2] + c * 256, OFF[2] + (c + 1) * 256),
            np.arange(OFF[0] + c * 128, OFF[0] + (c + 1) * 128),
            np.arange(OFF[3] + c * 256, OFF[3] + (c + 1) * 256),
        ])
        wh = np.ascontiguousarray(w_in[:, :, cols])
        hp = np.zeros((128, DEPTH, 8), np.float32)
        hp[:, :, 0] = rdf[None, :, c]
        hp[:, :, 1] = rdb[None, :, c]
        hp[:, :, 2] = qn.T
        hp[:, :, 3] = kn.T
        hp[:, :, 4:8] = dl.transpose(2, 0, 1)
        gnw = np.ascontiguousarray(np.broadcast_to(gn[None, :, c * 256:(c + 1) * 256], (128, DEPTH, 256)))
        m = {"xT": xc, "nw": nw, "wh": wh, "hp": hp, "gnw": gnw, "subw": subw, "cst": cst, "cb": cb, "cosT": cosT, "sinT": sinT}
        for n, a in full.items():
            R = a.shape[1] // NCORES
            m[n + "_s"] = np.ascontiguousarray(a[:, c * R:(c + 1) * R, :])
        maps.append(m)
    return maps


def kernel(**inputs):
    nc, _ = build(DEPTH, False)
    maps = make_in_maps(**inputs)
    res = run_bass_kernel_spmd(nc, maps, core_ids=list(range(NCORES)))
    outs = []
    for c in range(NCORES):
        o = np.asarray(res.results[c]["out"], dtype=np.float32)
        outs.append(o.transpose(1, 0, 2).reshape(D, TOK).T)
    return np.concatenate(outs, axis=0)[None].astype(np.float32)
```
